# Optimizing a Trainium2 kernel written in Bass

```python
import math
import jax, jax.numpy as jnp
from jax import lax
import numpy as np

D_MODEL = 2048
BATCH = 1
SEQ = 16384
DEPTH = 4

GRID_W = 64
CTX_LEN = 256
A_HEADS = 8
A_QK_DIM = 64
A_V_DIM = 2 * A_QK_DIM
A_WIDTH = A_HEADS * A_V_DIM
ROPE_FREQS = A_QK_DIM // 4
ROPE_BASE = 10000.0
Q_BLOCK = 128
B_GROUPS = 8
B_CHUNK = 128
B_GROUP_DIM = 128
B_WIDTH = B_GROUPS * B_GROUP_DIM
NORM_EPS = 1e-6
IN_SIZES = (A_HEADS * 2 * A_QK_DIM, A_HEADS * 2 * A_QK_DIM, A_WIDTH, A_WIDTH,
            B_WIDTH, B_WIDTH, B_WIDTH, 2 * D_MODEL)
IN_WIDTH = sum(IN_SIZES)

kernel_name = "hybrid_diffattn_chunkgmlp_prefix_dit"


def _rmsnorm(x, g):
    xf = x.astype(jnp.float32)
    y = xf * lax.rsqrt(jnp.mean(xf * xf, axis=-1, keepdims=True) + NORM_EPS)
    return (y * g.astype(jnp.float32)).astype(x.dtype)


def _lambda_init(layer):
    return 0.8 - 0.6 * math.exp(-0.3 * layer)


def _axial_rope_tables(row, col, dtype):
    inv = ROPE_BASE ** (-jnp.arange(ROPE_FREQS, dtype=jnp.float32) / ROPE_FREQS)
    ang = jnp.stack([row.astype(jnp.float32)[:, None] * inv,
                     col.astype(jnp.float32)[:, None] * inv], axis=1)
    return jnp.cos(ang).astype(dtype), jnp.sin(ang).astype(dtype)


def _apply_rope(x, cos, sin):
    b, t, h, m, _ = x.shape
    xr = x.reshape(b, t, h, m, 2, 2, ROPE_FREQS)
    x1, x2 = xr[..., 0, :], xr[..., 1, :]
    cs = cos[None, :, None, None]
    sn = sin[None, :, None, None]
    out = jnp.stack([x1 * cs - x2 * sn, x2 * cs + x1 * sn], axis=-2)
    return out.reshape(x.shape)


def _project(h, shift, scale, norm_g, w_in, q_norm_g, k_norm_g):
    hn = _rmsnorm(h, norm_g) * (1.0 + scale) + shift
    z = hn @ w_in
    cuts, acc = [], 0
    for s in IN_SIZES[:-1]:
        acc += s
        cuts.append(acc)
    q, k, v, ga, ub, vb, gb, mg = jnp.split(z, cuts, axis=-1)
    b, t = h.shape[0], h.shape[1]
    q = _rmsnorm(q.reshape(b, t, A_HEADS, 2, A_QK_DIM), q_norm_g)
    k = _rmsnorm(k.reshape(b, t, A_HEADS, 2, A_QK_DIM), k_norm_g)
    v = v.reshape(b, t, A_HEADS, A_V_DIM)
    return q, k, v, ga, ub, vb, gb, mg


def _pair_attend(q, k, v):
    s = jnp.einsum('bqhmd,bkhmd->bhmqk', q, k).astype(jnp.float32) * (A_QK_DIM ** -0.5)
    p = jax.nn.softmax(s, axis=-1).astype(v.dtype)
    return jnp.einsum('bhmqk,bkhe->bqhme', p, v)


def _latent_attend(q, k_all, v_all):
    b, t = q.shape[0], q.shape[1]
    qb = q.reshape(b, t // Q_BLOCK, Q_BLOCK, A_HEADS, 2, A_QK_DIM).swapaxes(0, 1)
    o = lax.map(lambda qi: _pair_attend(qi, k_all, v_all), qb)
    return o.swapaxes(0, 1).reshape(b, t, A_HEADS, 2, A_V_DIM)


def _diff_heads(o, lam, lam_init, subln_g):
    d = o[..., 0, :] - lam.astype(o.dtype) * o[..., 1, :]
    d = _rmsnorm(d, subln_g) * (1.0 - lam_init)
    return d.reshape(d.shape[0], d.shape[1], A_WIDTH)


def _chunk_gmlp(ub, vb, v_norm_g, w_spatial, b_spatial):
    b, t, _ = ub.shape
    u = jax.nn.gelu(ub)
    v = _rmsnorm(jax.nn.gelu(vb).reshape(b, t // B_CHUNK, B_CHUNK, B_GROUPS, B_GROUP_DIM), v_norm_g)
    s = jnp.einsum('gpq,bnqgc->bnpgc', w_spatial, v) + b_spatial.T[None, None, :, :, None]
    return u * s.reshape(b, t, B_WIDTH)


def _merge(a, ga, bo, gb, mg, w_proj_a, w_proj_b, w_out):
    ya = (a * jax.nn.silu(ga)) @ w_proj_a
    yb = (bo * jax.nn.silu(gb)) @ w_proj_b
    ma, mb = jnp.split(jax.nn.sigmoid(mg), 2, axis=-1)
    return (ma * ya + mb * yb) @ w_out


def setup_inputs(seed: int = 0) -> dict:
    key = jax.random.key(seed)
    ks = jax.random.split(key, 24)
    f32 = jnp.float32

    def nrm(k, shape, s):
        return jax.random.normal(k, shape, f32) * s

    return {
        "x": nrm(ks[0], (BATCH, SEQ, D_MODEL), 1.0),
        "c": nrm(ks[1], (BATCH, D_MODEL), 1.0),
        "ctx": nrm(ks[2], (BATCH, CTX_LEN, D_MODEL), 1.0),
        "c_ctx": nrm(ks[3], (D_MODEL,), 1.0),
        "w_ada": nrm(ks[4], (DEPTH, D_MODEL, 3 * D_MODEL), D_MODEL ** -0.5),
        "b_ada": nrm(ks[5], (DEPTH, 3 * D_MODEL), 0.02),
        "norm_g": 1.0 + nrm(ks[6], (DEPTH, D_MODEL), 0.05),
        "w_in": nrm(ks[7], (DEPTH, D_MODEL, IN_WIDTH), D_MODEL ** -0.5),
        "q_norm_g": 1.0 + nrm(ks[8], (DEPTH, A_QK_DIM), 0.05),
        "k_norm_g": 1.0 + nrm(ks[9], (DEPTH, A_QK_DIM), 0.05),
        "lam_q1": nrm(ks[10], (DEPTH, A_QK_DIM), 0.1),
        "lam_k1": nrm(ks[11], (DEPTH, A_QK_DIM), 0.1),
        "lam_q2": nrm(ks[12], (DEPTH, A_QK_DIM), 0.1),
        "lam_k2": nrm(ks[13], (DEPTH, A_QK_DIM), 0.1),
        "subln_g": 1.0 + nrm(ks[14], (DEPTH, A_V_DIM), 0.05),
        "v_norm_g": 1.0 + nrm(ks[15], (DEPTH, B_GROUPS, B_GROUP_DIM), 0.05),
        "w_spatial": nrm(ks[16], (DEPTH, B_GROUPS, B_CHUNK, B_CHUNK), 0.5 * B_CHUNK ** -0.5),
        "b_spatial": 1.0 + nrm(ks[17], (DEPTH, B_GROUPS, B_CHUNK), 0.1),
        "w_proj_a": nrm(ks[18], (DEPTH, A_WIDTH, D_MODEL), A_WIDTH ** -0.5),
        "w_proj_b": nrm(ks[19], (DEPTH, B_WIDTH, D_MODEL), B_WIDTH ** -0.5),
        "w_out": nrm(ks[20], (DEPTH, D_MODEL, D_MODEL), D_MODEL ** -0.5),
    }


def reference(x, c, ctx, c_ctx, w_ada, b_ada, norm_g, w_in, q_norm_g, k_norm_g,
              lam_q1, lam_k1, lam_q2, lam_k2, subln_g, v_norm_g, w_spatial, b_spatial,
              w_proj_a, w_proj_b, w_out):
    seq = x.shape[1]
    rows = seq // GRID_W
    row = jnp.repeat(jnp.arange(rows, dtype=jnp.int32), GRID_W)
    col = jnp.tile(jnp.arange(GRID_W, dtype=jnp.int32), rows)
    cos, sin = _axial_rope_tables(row, col, x.dtype)

    silu_c = jax.nn.silu(c)
    silu_cc = jax.nn.silu(c_ctx)
    h, hc = x, ctx
    for layer in range(DEPTH):
        mod = silu_c @ w_ada[layer] + b_ada[layer]
        shift, scale, gate = [m[:, None, :] for m in jnp.split(mod, 3, axis=-1)]
        mod_c = silu_cc @ w_ada[layer] + b_ada[layer]
        shift_c, scale_c, gate_c = jnp.split(mod_c, 3)
        lam_init = _lambda_init(layer)
        lam = (jnp.exp(jnp.sum(lam_q1[layer] * lam_k1[layer]).astype(jnp.float32))
               - jnp.exp(jnp.sum(lam_q2[layer] * lam_k2[layer]).astype(jnp.float32))
               + lam_init)

        qc, kc, vc, gac, ubc, vbc, gbc, mgc = _project(
            hc, shift_c, scale_c, norm_g[layer], w_in[layer], q_norm_g[layer], k_norm_g[layer])
        q, k, v, ga, ub, vb, gb, mg = _project(
            h, shift, scale, norm_g[layer], w_in[layer], q_norm_g[layer], k_norm_g[layer])
        q = _apply_rope(q, cos, sin)
        k = _apply_rope(k, cos, sin)

        k_all = jnp.concatenate([kc, k], axis=1)
        v_all = jnp.concatenate([vc, v], axis=1)
        a_lat = _diff_heads(_latent_attend(q, k_all, v_all), lam, lam_init, subln_g[layer])
        b_lat = _chunk_gmlp(ub, vb, v_norm_g[layer], w_spatial[layer], b_spatial[layer])
        out = _merge(a_lat, ga, b_lat, gb, mg, w_proj_a[layer], w_proj_b[layer], w_out[layer])
        h = h + gate * out

        if layer < DEPTH - 1:
            a_ctx = _diff_heads(_pair_attend(qc, kc, vc), lam, lam_init, subln_g[layer])
            b_ctx = _chunk_gmlp(ubc, vbc, v_norm_g[layer], w_spatial[layer], b_spatial[layer])
            out_c = _merge(a_ctx, gac, b_ctx, gbc, mgc, w_proj_a[layer], w_proj_b[layer], w_out[layer])
            hc = hc + gate_c * out_c
    return h
```

```python
import math
from contextlib import ExitStack

import numpy as np
import ml_dtypes
import concourse.bass as bass
import concourse.mybir as mybir
from concourse.bass_utils import run_bass_kernel_spmd

F32 = mybir.dt.float32
BF16 = mybir.dt.bfloat16
AF = mybir.ActivationFunctionType
ALU = mybir.AluOpType
AX = mybir.AxisListType

NCORES = 8
D = 2048
SEQ = 16384
CTX = 256
DEPTH = 4
TOK = SEQ // NCORES
T = CTX + TOK
NT = T // 128
NKEY = CTX + SEQ
NKC = NKEY // 128
KC = D // 128
INW = 11264
EPS = 1e-6
GRID_W = 64


class Sem:
    def __init__(self, h):
        self.h = h
        self.cnt = 0


class Buf:
    def __init__(self, kb, name=None):
        self.kb = kb
        self.w = {}
        self.r = {}
        self.dsem = None
        self.name = name
        self.untracked = False
        self.multi = False

    def get_dsem(self):
        if self.dsem is None:
            kb = self.kb
            if kb.dsem_idx >= len(kb.dsem_pool):
                kb.dsem_pool.append(kb.new_sem("d"))
            self.dsem = kb.dsem_pool[kb.dsem_idx]
            kb.dsem_idx += 1
        return self.dsem


def _merge(d, s):
    for sem, val in s.items():
        if d.get(sem, 0) < val:
            d[sem] = val


class Stream:
    def __init__(self, kb, eng, name):
        self.kb = kb
        self.eng = eng
        self.sem = kb.new_sem("s_" + name)
        self.waited = {}
        self.name = name
        self.in_order = (name == "pe")

    def _wait(self, reads, writes, extra=None):
        reads = [b for b in reads if not b.untracked]
        writes = [b for b in writes if not b.untracked and not b.multi]
        deps = {}
        for b in reads:
            _merge(deps, b.w)
        for b in writes:
            _merge(deps, b.w)
            _merge(deps, b.r)
        if extra:
            _merge(deps, extra)
        for sem, val in deps.items():
            if sem is self.sem and (self.in_order or self.sem.cnt - val >= 3):
                continue
            if self.waited.get(sem, 0) < val:
                self.eng.wait_ge(sem.h, val)
                self.waited[sem] = val

    def _record(self, tok, reads, writes):
        reads = [b for b in reads if not b.untracked]
        writes = [b for b in writes if not b.untracked]
        sem, val = tok
        for b in reads:
            if b.r.get(sem, 0) < val:
                b.r[sem] = val
        for b in writes:
            if b.multi:
                if b.w.get(sem, 0) < val:
                    b.w[sem] = val
                continue
            b.w = {sem: val}
            b.r = {}

    def op(self, fn, reads=(), writes=()):
        self._wait(reads, writes)
        inst = fn()
        self.sem.cnt += 1
        inst.then_inc(self.sem.h, 1)
        tok = (self.sem, self.sem.cnt)
        self._record(tok, reads, writes)
        return tok

    def group(self, fns, reads=(), writes=()):
        self._wait(reads, writes)
        inst = None
        for fn in fns:
            inst = fn()
        self.sem.cnt += 1
        inst.then_inc(self.sem.h, 1)
        tok = (self.sem, self.sem.cnt)
        self._record(tok, reads, writes)
        return tok

    def dma(self, out, in_, reads=(), writes=(), sbuf_buf=None, **kw):
        self._wait(reads, writes)
        ds = sbuf_buf.get_dsem()
        inst = self.eng.dma_start(out=out, in_=in_, **kw)
        ds.cnt += 16
        inst.then_inc(ds.h, 16)
        tok = (ds, ds.cnt)
        self._record(tok, reads, writes)
        return tok

    def wait_all(self, bufs):
        self._wait(bufs, bufs)


class KB:
    def __init__(self):
        self.nc = bass.Bass("TRN2", target_bir_lowering=False)
        self.es = ExitStack()
        self.nsem = 0
        self.bufs = []
        self.all_sems = []
        self.dsem_pool = []
        self.dsem_idx = 0
        nc = self.nc
        self.sem_pool = [nc.alloc_semaphore(name=f"sm{i}") for i in range(100)]
        for h in self.sem_pool:
            nc.gpsimd.sem_clear(h)
        nc.all_engine_barrier()
        self.pe = Stream(self, nc.tensor, "pe")
        self.act = Stream(self, nc.scalar, "act")
        self.dve = Stream(self, nc.vector, "dve")
        self.pool = Stream(self, nc.gpsimd, "pool")
        self.sp = Stream(self, nc.sync, "sp")

    def new_sem(self, name):
        self.nsem += 1
        sm = Sem(self.sem_pool[self.nsem - 1])
        self.all_sems.append(sm)
        return sm

    def barrier(self):
        for st in (self.pe, self.act, self.dve, self.pool, self.sp):
            for sm in self.all_sems:
                if sm.cnt > st.waited.get(sm, 0):
                    st.eng.wait_ge(sm.h, sm.cnt)
                    st.waited[sm] = sm.cnt
        self.dsem_idx = 0
        for b in self.bufs:
            b.dsem = None

    def allgather(self, in_ap, out_ap, reads, writes):
        st = self.pool
        st._wait(reads, writes)
        sm = self.new_sem("cc")
        self.all_sems.remove(sm)
        inst = self.nc.gpsimd.collective_compute("AllGather", ALU.bypass, replica_groups=[list(range(NCORES))],
                                                 ins=[in_ap], outs=[out_ap])
        inst.then_inc(sm.h)
        sm.cnt = 1
        st._record((sm, 1), reads, writes)

    def sb(self, name, shape, dt):
        return self.es.enter_context(self.nc.sbuf_tensor(name, shape, dt))

    def ps(self, name, shape, dt=F32):
        return self.es.enter_context(self.nc.psum_tensor(name, shape, dt))

    def dram_in(self, name, shape, dt=F32):
        return self.nc.dram_tensor(name, list(shape), dt, kind="ExternalInput").ap()

    def dram_out(self, name, shape, dt=F32):
        return self.nc.dram_tensor(name, list(shape), dt, kind="ExternalOutput").ap()

    def buf(self, name=None):
        b = Buf(self, name)
        self.bufs.append(b)
        return b


def bcast_free(ap_, shape_pattern):
    a = ap_.ap
    return bass.AP(ap_.tensor, ap_.offset, [list(a[0])] + [list(x) for x in shape_pattern])


class StopEmit(Exception):
    pass


def emit_phase_a(kb, io, stop_after=None, pfx="A"):
    nc = kb.nc

    def ck(name):
        if stop_after == name:
            raise StopEmit()

    pe, act, dve, pool, sp = kb.pe, kb.act, kb.dve, kb.pool, kb.sp
    hin, norm_g, w_in = io["hin"], io["norm_g"], io["w_in"]
    qk_g, v_norm_g, w_spatial, b_spatial, w_proj_b = io["qk_g"], io["v_norm_g"], io["w_spatial"], io["b_spatial"], io["w_proj_b"]
    cos_d, sin_d = io["cos"], io["sin"]
    QT, KT, V, gaT, tbT, maT = io["QT"], io["KT"], io["V"], io["gaT"], io["tbT"], io["maT"]

    es = ExitStack()

    def sb(name, shape, dt):
        return es.enter_context(nc.sbuf_tensor(pfx + "sb_" + name, shape, dt))

    def ps(name, shape, dt=F32):
        return es.enter_context(nc.psum_tensor(pfx + "ps_" + name, shape, dt))

    hnT = sb("hnT", [128, KC, T], BF16)
    hnT_b = [kb.buf(f"hnT{t}") for t in range(NT)]
    wg = [sb(f"wg{i}", [128, KC, 512], BF16) for i in range(2)]
    wg_b = [kb.buf(f"wg{i}") for i in range(2)]
    wpb = [sb(f"wpb{i}", [128, 8, 512], BF16) for i in range(2)]
    wpb_b = [kb.buf(f"wpb{i}") for i in range(2)]
    uT = sb("uT", [128, 8, T], BF16)
    uT_b = kb.buf("uT")
    uT_f = uT[:].rearrange("p g t -> p (g t)").bitcast(F32)
    ident = sb("ident", [128, 128], F32)
    ident_bf = sb("identbf", [128, 128], BF16)
    ident_b = kb.buf("ident")
    ones_f = sb("ones_f", [1, 128], F32)
    cos_t = sb("cos_t", [128, NT, 32], F32)
    sin_t = sb("sin_t", [128, NT, 32], F32)
    tab_b = kb.buf("tab")
    qkg_t = sb("qkg_t", [128, 2, 64], F32)
    vng_t = sb("vng_t", [128, 1024], F32)
    wsT = sb("wsT", [128, 8, 128], BF16)
    wsT_b = kb.buf("wsT")
    bsp_t = sb("bsp_t", [1, 8, 128], F32)
    cst_b = kb.buf("cst")
    modT = sb("modT", [128, 48, 2], F32)
    gsT = sb("gsT", [128, KC, 2], F32)
    ngT = sb("ngT", [128, KC], F32)
    mod_b = kb.buf("mod")
    pacc = [ps(f"pacc{i}", [128, 512]) for i in range(4)]
    pacc_b = [kb.buf(f"pacc{i}") for i in range(4)]
    ptr = [ps(f"ptr{i}", [128, 512]) for i in range(2)]
    ptr_b = [kb.buf(f"ptr{i}") for i in range(2)]
    pms = [ps(f"pms{i}", [128, 512]) for i in range(2)]
    pms_b = [kb.buf(f"pms{i}") for i in range(2)]
    ptr_bf = [p[:].bitcast(BF16) for p in ptr]

    eps_t = sb("eps_t", [128, 1], F32)
    eps_b = kb.buf("eps")
    dve.op(lambda: nc.vector.memset(eps_t[:], EPS), writes=[eps_b])

    def rstd_inplace(ap_, b_, invn):
        np_ = ap_.shape[0]
        act.op(lambda: nc.scalar.activation(out=ap_, in_=ap_, func=AF.Sqrt, scale=invn, bias=eps_t[0:np_, :]), reads=[b_, eps_b], writes=[b_])
        dve.op(lambda: nc.vector.reciprocal(out=ap_, in_=ap_), reads=[b_], writes=[b_])

    sp.dma(ident[:], io["ident"], writes=[ident_b], sbuf_buf=ident_b)
    dve.op(lambda: nc.vector.tensor_copy(out=ident_bf[:], in_=ident[:]), reads=[ident_b], writes=[kb.buf()])
    dve.op(lambda: nc.vector.memset(ones_f[:], 1.0), writes=[cst_b])
    sp.dma(cos_t[:], cos_d.rearrange("(t p) f -> p t f", p=128), writes=[tab_b], sbuf_buf=tab_b)
    sp.dma(sin_t[:], sin_d.rearrange("(t p) f -> p t f", p=128), writes=[tab_b], sbuf_buf=tab_b)
    sp.dma(qkg_t[:].rearrange("p a d -> p (a d)"), qk_g.rearrange("a d -> (a d)").partition_broadcast(128), writes=[cst_b], sbuf_buf=cst_b)
    sp.dma(vng_t[:], v_norm_g.rearrange("a d -> (a d)").partition_broadcast(128), writes=[cst_b], sbuf_buf=cst_b)
    sp.dma(bsp_t[:], b_spatial.rearrange("(o g) p -> o g p", o=1), writes=[cst_b], sbuf_buf=cst_b)
    with nc.allow_non_contiguous_dma("tiny norm_g transpose load"):
        sp.dma(ngT[:], norm_g.rearrange("o (k p) -> p (o k)", p=128), writes=[mod_b], sbuf_buf=mod_b)

    wsp_f = uT_f[:, 8192:9216].rearrange("p (g q) -> p g q", g=8)
    wspf_b = kb.buf("wspf")
    sp.dma(wsp_f, w_spatial.rearrange("g p q -> p g q"), writes=[wspf_b], sbuf_buf=wspf_b)
    for half in range(2):
        pb = ptr_b[half]
        pe.group([(lambda g=g: nc.tensor.transpose(ptr[half][:, (g % 4) * 128:(g % 4 + 1) * 128], wsp_f[:, g, :], ident[:]))
                  for g in range(half * 4, half * 4 + 4)], reads=[wspf_b, ident_b], writes=[pb])
        dve.op(lambda: nc.vector.tensor_copy(out=wsT[:, half * 4:half * 4 + 4, :].rearrange("p g q -> p (g q)"), in_=ptr[half][:]),
               reads=[pb], writes=[wsT_b])

    ck("const")
    wq = [0]

    def load_group(src_ap, wb):
        i = wq[0] % 2
        wq[0] += 1
        pool.dma(wg[i][:], src_ap.rearrange("(k p) n -> p k n", p=128), reads=[wb], writes=[wg_b[i]], sbuf_buf=wg_b[i])
        return i

    pa = [0]

    def next_pacc():
        i = pa[0] % 4
        pa[0] += 1
        return i

    modrows = [sb(f"modrows{i}", [2, 384], F32) for i in range(2)]
    modrows_b = [kb.buf(f"modrows{i}") for i in range(2)]
    modall = io["modall"]
    lyr = io["layer"]
    for g in range(16):
        bi = g % 2
        c_, half = g // 2, g % 2
        r0 = c_ * 2 * DEPTH + lyr * 2
        sp.dma(modrows[bi][:], modall[r0:r0 + 2, half * 384:(half + 1) * 384], reads=[io["modall_b"]], writes=[modrows_b[bi]], sbuf_buf=modrows_b[bi])
        pe.group([(lambda j=j: nc.tensor.transpose(ptr[1][:, 2 * (3 * g + j):2 * (3 * g + j) + 2], modrows[bi][:, j * 128:(j + 1) * 128], ident[0:2, 0:2]))
                  for j in range(3)], reads=[modrows_b[bi], ident_b], writes=[ptr_b[1]])
    dve.op(lambda: nc.vector.tensor_copy(out=modT[:].rearrange("p k v -> p (k v)"), in_=ptr[1][:, 0:96]),
           reads=[ptr_b[1]], writes=[mod_b])
    dve.op(lambda: nc.vector.scalar_tensor_tensor(out=gsT[:], in0=modT[:, 16:32, :], scalar=1.0,
                                                  in1=bcast_free(ngT[:], [[1, KC], [0, 2]]), op0=ALU.add, op1=ALU.mult),
           reads=[mod_b], writes=[mod_b])

    ck("a0")
    xt = [uT_f[:, i * D:(i + 1) * D] for i in range(2)]
    xt_b = [kb.buf(f"xt{i}") for i in range(2)]
    junk = uT_f[:, 2 * D:3 * D]
    junk_b = kb.buf("junk")
    ssq = [sb(f"ssq{i}", [128, 1], F32) for i in range(2)]
    ssq_b = [kb.buf(f"ssq{i}") for i in range(2)]
    tr_i = [0]
    for t in range(NT):
        i = t % 2
        var = 1 if t < CTX // 128 else 0
        sp.dma(xt[i], hin[t * 128:(t + 1) * 128, :], reads=[io["h_b"]], writes=[xt_b[i]], sbuf_buf=xt_b[i])
        act.op(lambda: nc.scalar.activation(out=junk, in_=xt[i], func=AF.Square), reads=[xt_b[i]], writes=[junk_b])
        dve.op(lambda: nc.vector.tensor_reduce(out=ssq[i][:], in_=junk, axis=AX.X, op=ALU.add), reads=[junk_b], writes=[ssq_b[i]])
        if stop_after == "a1s1":
            raise StopEmit()
        rstd_inplace(ssq[i][:], ssq_b[i], 1.0 / D)
        if stop_after == "a1s2":
            raise StopEmit()
        dve.op(lambda: nc.vector.tensor_scalar(out=xt[i], in0=xt[i], scalar1=ssq[i][:, 0:1], scalar2=None, op0=ALU.mult),
               reads=[xt_b[i], ssq_b[i]], writes=[xt_b[i]])
        if stop_after == "a1s3":
            raise StopEmit()
        for q4 in range(4):
            pi = tr_i[0] % 2
            tr_i[0] += 1
            pe.group([(lambda k=k: nc.tensor.transpose(ptr[pi][:, (k % 4) * 128:(k % 4 + 1) * 128], xt[i][:, k * 128:(k + 1) * 128], ident[:]))
                      for k in range(q4 * 4, q4 * 4 + 4)], reads=[xt_b[i], ident_b], writes=[ptr_b[pi]])
            for k in range(q4 * 4, q4 * 4 + 4):
                dve.op(lambda k=k: nc.vector.tensor_scalar(out=hnT[:, k, t * 128:(t + 1) * 128], in0=ptr[pi][:, (k % 4) * 128:(k % 4 + 1) * 128],
                                                           scalar1=gsT[:, k, var:var + 1], scalar2=modT[:, k, var:var + 1],
                                                           op0=ALU.mult, op1=ALU.add),
                       reads=[ptr_b[pi], mod_b], writes=[hnT_b[t]])

    ck("a1")
    for b_ in xt_b + [junk_b, wspf_b]:
        _merge(uT_b.w, b_.w)
        _merge(uT_b.r, b_.r)
    tq_sets = [[sb(f"tq{s_}_{i}", [128, 512], F32) for i in range(3)] for s_ in range(2)]
    tqb_sets = [[kb.buf(f"tq{s_}_{i}") for i in range(3)] for s_ in range(2)]
    tq = list(tq_sets[0])
    tq_b = list(tqb_sets[0])
    flipc = [0]

    def flip():
        flipc[0] ^= 1
        tq[:] = tq_sets[flipc[0]]
        tq_b[:] = tqb_sets[flipc[0]]

    sm = [sb(f"sm{i}", [128, 8], F32) for i in range(2)]
    sm_b = [kb.buf(f"sm{i}") for i in range(2)]
    zr = [sb(f"zr{i}", [128, 512], BF16) for i in range(2)]
    zr_b = [kb.buf(f"zr{i}") for i in range(2)]
    stg = [sb(f"stg{i}", [128, 4, 512], BF16) for i in range(2)]
    stg_b = [kb.buf(f"stg{i}") for i in range(2)]
    vst = [sb(f"vst{i}", [128, 512], BF16) for i in range(2)]
    vst_b = [kb.buf(f"vst{i}") for i in range(2)]
    cnt = {"zr": 0, "stg": 0, "vst": 0, "tr": 0, "ms": 0, "sm": 0}

    def rr(key, n):
        i = cnt[key] % n
        cnt[key] += 1
        return i

    def tok_major_unit(cur, t):
        pi = next_pacc()
        pe.group([(lambda k=k: nc.tensor.matmul(pacc[pi][:], hnT[:, k, t * 128:(t + 1) * 128], wg[cur][:, k, :],
                                                start=(k == 0), stop=(k == KC - 1))) for k in range(KC)],
                 reads=[hnT_b[t], wg_b[cur]], writes=[pacc_b[pi]])
        return pi

    TB = [(0, 512), (512, 512), (1024, 512), (1536, 512), (2048, 256)]

    def feat_major_unit(cur, j, tb):
        t0, n = TB[tb]
        pi = next_pacc()
        tiles = [hnT_b[t] for t in range(t0 // 128, (t0 + n) // 128)]
        pe.group([(lambda k=k: nc.tensor.matmul(pacc[pi][:, 0:n], wg[cur][:, k, j * 128:(j + 1) * 128], hnT[:, k, t0:t0 + n],
                                                start=(k == 0), stop=(k == KC - 1))) for k in range(KC)],
                 reads=tiles + [wg_b[cur]], writes=[pacc_b[pi]])
        return pi

    def v4(ap2d):
        return ap2d.rearrange("p (a x h f) -> p a x h f", a=8, x=2, h=2, f=16)

    def qk_group(col0, which, out_d, head0):
        cur = load_group_pref(col0)
        for t4 in range(0, NT, 4):
            tl = list(range(t4, min(t4 + 4, NT)))
            si = rr("stg", 2)
            for t in tl:
                pi = tok_major_unit(cur, t)
                flip()
                a, b, c = tq[0], tq[1], tq[2]
                smi = rr("sm", 2)
                act.op(lambda: nc.scalar.activation(out=a[:], in_=pacc[pi][:], func=AF.Square), reads=[pacc_b[pi]], writes=[tq_b[0]])
                dve.op(lambda: nc.vector.tensor_reduce(out=sm[smi][:], in_=a[:].rearrange("p (a d) -> p a d", d=64), axis=AX.X, op=ALU.add),
                       reads=[tq_b[0]], writes=[sm_b[smi]])
                rstd_inplace(sm[smi][:], sm_b[smi], 1.0 / 64)
                dve.op(lambda: nc.vector.tensor_tensor(out=b[:].rearrange("p (a d) -> p a d", d=64), in0=pacc[pi][:].rearrange("p (a d) -> p a d", d=64),
                                                       in1=bcast_free(sm[smi][:], [[1, 8], [0, 64]]), op=ALU.mult),
                       reads=[pacc_b[pi], sm_b[smi]], writes=[tq_b[1]])
                dve.op(lambda: nc.vector.tensor_tensor(out=b[:].rearrange("p (a d) -> p a d", d=64), in0=b[:].rearrange("p (a d) -> p a d", d=64),
                                                       in1=bcast_free(qkg_t[:, which, :], [[0, 8], [1, 64]]), op=ALU.mult),
                       reads=[tq_b[1], cst_b], writes=[tq_b[1]])
                x1 = v4(b[:])[:, :, :, 0, :]
                x2 = v4(b[:])[:, :, :, 1, :]
                cs = bcast_free(cos_t[:, t, :], [[0, 8], [16, 2], [1, 16]])
                sn = bcast_free(sin_t[:, t, :], [[0, 8], [16, 2], [1, 16]])
                zi = rr("zr", 2)
                o1 = v4(zr[zi][:])[:, :, :, 0, :]
                o2 = v4(zr[zi][:])[:, :, :, 1, :]
                a1 = v4(a[:])[:, :, :, 0, :]
                a2 = v4(a[:])[:, :, :, 1, :]
                c1 = v4(c[:])[:, :, :, 0, :]
                c2 = v4(c[:])[:, :, :, 1, :]
                pool.op(lambda: nc.gpsimd.tensor_tensor(out=a1, in0=x1, in1=cs, op=ALU.mult), reads=[tq_b[1], tab_b], writes=[tq_b[0]])
                pool.op(lambda: nc.gpsimd.tensor_tensor(out=a2, in0=x2, in1=cs, op=ALU.mult), reads=[tq_b[1], tab_b], writes=[tq_b[0]])
                dve.op(lambda: nc.vector.tensor_tensor(out=c1, in0=x2, in1=sn, op=ALU.mult), reads=[tq_b[1], tab_b], writes=[tq_b[2]])
                dve.op(lambda: nc.vector.tensor_tensor(out=c2, in0=x1, in1=sn, op=ALU.mult), reads=[tq_b[1], tab_b], writes=[tq_b[2]])
                dve.op(lambda: nc.vector.tensor_tensor(out=o1, in0=a1, in1=c1, op=ALU.subtract), reads=[tq_b[0], tq_b[2]], writes=[zr_b[zi]])
                dve.op(lambda: nc.vector.tensor_tensor(out=o2, in0=a2, in1=c2, op=ALU.add), reads=[tq_b[0], tq_b[2]], writes=[zr_b[zi]])
                ti = rr("tr", 2)
                pe.group([(lambda hh=hh: nc.tensor.transpose(ptr_bf[ti][:, hh * 128:(hh + 1) * 128], zr[zi][:, hh * 128:(hh + 1) * 128], ident_bf[:]))
                          for hh in range(4)], reads=[zr_b[zi], ident_b], writes=[ptr_b[ti]])
                tt = t - t4
                act.op(lambda: nc.scalar.copy(out=stg[si][:, :, tt * 128:(tt + 1) * 128],
                                              in_=ptr_bf[ti][:, 0:512].rearrange("p (h q) -> p h q", h=4)),
                       reads=[ptr_b[ti]], writes=[stg_b[si]])
            n = len(tl) * 128
            if which == 0:
                sp.dma(out_d[head0:head0 + 4, :, t4 * 128:t4 * 128 + n].rearrange("h p q -> p h q"), stg[si][:, :, 0:n],
                       reads=[stg_b[si]], writes=[io["QT_b"]], sbuf_buf=stg_b[si])
            else:
                c0 = t4 * 128
                if c0 < CTX:
                    sp.dma(io["KTc"][head0:head0 + 4, :, c0:CTX].rearrange("h p q -> p h q"), stg[si][:, :, 0:CTX - c0],
                           reads=[stg_b[si]], writes=[io["KV_b"]], sbuf_buf=stg_b[si])
                    sp.dma(out_d[head0:head0 + 4, :, 0:c0 + n - CTX].rearrange("h p q -> p h q"), stg[si][:, :, CTX - c0:n],
                           reads=[stg_b[si]], writes=[io["KV_b"]], sbuf_buf=stg_b[si])
                else:
                    sp.dma(out_d[head0:head0 + 4, :, c0 - CTX:c0 - CTX + n].rearrange("h p q -> p h q"), stg[si][:, :, 0:n],
                           reads=[stg_b[si]], writes=[io["KV_b"]], sbuf_buf=stg_b[si])

    order = []
    for c0 in range(0, 2048, 512):
        order.append(c0)
    for c0 in range(2048, INW, 512):
        order.append(c0)
    pending = {}
    pos = [0]

    def load_group_pref(col0):
        if col0 not in pending:
            pending[col0] = load_group(w_in[:, col0:col0 + 512], io["wb_w_in"])
        cur = pending.pop(col0)
        idx = order.index(col0)
        if idx + 1 < len(order):
            nx = order[idx + 1]
            pending[nx] = load_group(w_in[:, nx:nx + 512], io["wb_w_in"])
        return cur

    qk_group(0, 0, QT, 0)
    ck("qk1")
    qk_group(512, 0, QT, 4)
    qk_group(1024, 1, KT, 0)
    qk_group(1536, 1, KT, 4)

    ck("qk")
    for gi, c0 in enumerate((2048, 2560)):
        cur = load_group_pref(c0)
        for t in range(NT):
            pi = tok_major_unit(cur, t)
            vi = rr("vst", 2)
            act.op(lambda: nc.scalar.copy(out=vst[vi][:], in_=pacc[pi][:]), reads=[pacc_b[pi]], writes=[vst_b[vi]])
            if t < 2:
                dst = io["Vc"][gi * 4:gi * 4 + 4, :, t, :]
            else:
                dst = V[gi * 4:gi * 4 + 4, :, t - 2, :]
            sp.dma(dst.rearrange("h p e -> p h e"), vst[vi][:].rearrange("p (h e) -> p h e", h=4), reads=[vst_b[vi]], writes=[io["KV_b"]],
                   sbuf_buf=vst_b[vi])

    if io.get("after_kv") is not None:
        io["after_kv"]()
    ck("v")
    def gelu_from(src_ap, src_b, dst_ap, dst_b, n, shp=None):
        a, b = tq[0], tq[1]
        act.op(lambda: nc.scalar.activation(out=a[:, 0:n], in_=src_ap, func=AF.Square), reads=[src_b], writes=[tq_b[0]])
        dve.op(lambda: nc.vector.tensor_scalar(out=a[:, 0:n], in0=a[:, 0:n], scalar1=0.044715, scalar2=1.0, op0=ALU.mult, op1=ALU.add),
               reads=[tq_b[0]], writes=[tq_b[0]])
        dve.op(lambda: nc.vector.tensor_tensor(out=a[:, 0:n], in0=a[:, 0:n], in1=src_ap, op=ALU.mult), reads=[tq_b[0], src_b], writes=[tq_b[0]])
        act.op(lambda: nc.scalar.activation(out=b[:, 0:n], in_=a[:, 0:n], func=AF.Sigmoid, scale=2.0 * math.sqrt(2.0 / math.pi)),
               reads=[tq_b[0]], writes=[tq_b[1]])
        dve.op(lambda: nc.vector.tensor_tensor(out=dst_ap, in0=b[:, 0:n], in1=src_ap, op=ALU.mult), reads=[tq_b[1], src_b], writes=[dst_b])

    for gi, c0 in enumerate((3072, 3584)):
        cur = load_group_pref(c0)
        for tb in range(5):
            t0, n = TB[tb]
            si = rr("stg", 2)
            for j in range(4):
                pi = feat_major_unit(cur, j, tb)
                act.op(lambda: nc.scalar.activation(out=stg[si][:, j, 0:n], in_=pacc[pi][:, 0:n], func=AF.Silu),
                       reads=[pacc_b[pi]], writes=[stg_b[si]])
            sp.dma(gaT[gi * 512:(gi + 1) * 512, t0:t0 + n].rearrange("(j p) q -> p j q", p=128), stg[si][:, :, 0:n],
                   reads=[stg_b[si]], writes=[io["QT_b"]], sbuf_buf=stg_b[si])
    ck("ga")
    for gi, c0 in enumerate((4096, 4608)):
        cur = load_group_pref(c0)
        for tb in range(5):
            t0, n = TB[tb]
            for j in range(4):
                pi = feat_major_unit(cur, j, tb)
                flip()
                gelu_from(pacc[pi][:, 0:n], pacc_b[pi], uT[:, gi * 4 + j, t0:t0 + n], uT_b, n)
    ck("ub")
    vbt = [sb(f"vbt{i}", [128, 512], BF16) for i in range(2)]
    vbt_b = [kb.buf(f"vbt{i}") for i in range(2)]
    cnt["vbt"] = 0
    for gi, c0 in enumerate((5120, 5632)):
        cur = load_group_pref(c0)
        for t in range(NT):
            pi = tok_major_unit(cur, t)
            flip()
            c = tq[2]
            gelu_from(pacc[pi][:], pacc_b[pi], c[:], tq_b[2], 512)
            smi = rr("sm", 2)
            a = tq[0]
            act.op(lambda: nc.scalar.activation(out=a[:], in_=c[:], func=AF.Square), reads=[tq_b[2]], writes=[tq_b[0]])
            dve.op(lambda: nc.vector.tensor_reduce(out=sm[smi][:, 0:4], in_=a[:].rearrange("p (a d) -> p a d", d=128), axis=AX.X, op=ALU.add),
                   reads=[tq_b[0]], writes=[sm_b[smi]])
            rstd_inplace(sm[smi][:, 0:4], sm_b[smi], 1.0 / 128)
            dve.op(lambda: nc.vector.tensor_tensor(out=c[:].rearrange("p (a d) -> p a d", d=128), in0=c[:].rearrange("p (a d) -> p a d", d=128),
                                                   in1=bcast_free(sm[smi][:, 0:4], [[1, 4], [0, 128]]), op=ALU.mult),
                   reads=[tq_b[2], sm_b[smi]], writes=[tq_b[2]])
            vi = rr("vbt", 2)
            dve.op(lambda: nc.vector.tensor_tensor(out=vbt[vi][:], in0=c[:], in1=vng_t[:, gi * 512:(gi + 1) * 512], op=ALU.mult),
                   reads=[tq_b[2], cst_b], writes=[vbt_b[vi]])
            mi = rr("ms", 2)
            fns = []
            for j in range(4):
                g = gi * 4 + j
                fns.append(lambda j=j, g=g: nc.tensor.matmul(pms[mi][:, j * 128:(j + 1) * 128], vbt[vi][:, j * 128:(j + 1) * 128], wsT[:, g, :],
                                                             start=True, stop=False))
                fns.append(lambda j=j, g=g: nc.tensor.matmul(pms[mi][:, j * 128:(j + 1) * 128], ones_f[0:1, :], bsp_t[0:1, g, :],
                                                             start=False, stop=True))
            pe.group(fns, reads=[vbt_b[vi], wsT_b, cst_b], writes=[pms_b[mi]])
            dve.op(lambda: nc.vector.tensor_tensor(out=uT[:, gi * 4:gi * 4 + 4, t * 128:(t + 1) * 128],
                                                   in0=uT[:, gi * 4:gi * 4 + 4, t * 128:(t + 1) * 128],
                                                   in1=pms[mi][:].rearrange("p (j q) -> p j q", j=4), op=ALU.mult),
                   reads=[pms_b[mi], uT_b], writes=[uT_b])
    ck("vb")
    for gi, c0 in enumerate((6144, 6656)):
        cur = load_group_pref(c0)
        for tb in range(5):
            t0, n = TB[tb]
            for j in range(4):
                pi = feat_major_unit(cur, j, tb)
                flip()
                a = tq[0]
                act.op(lambda: nc.scalar.activation(out=a[:, 0:n], in_=pacc[pi][:, 0:n], func=AF.Silu), reads=[pacc_b[pi]], writes=[tq_b[0]])
                dve.op(lambda: nc.vector.tensor_tensor(out=uT[:, gi * 4 + j, t0:t0 + n], in0=uT[:, gi * 4 + j, t0:t0 + n], in1=a[:, 0:n], op=ALU.mult),
                       reads=[tq_b[0], uT_b], writes=[uT_b])
    ck("gb")
    for gi in range(4):
        cur = load_group_pref(7168 + gi * 512)
        for tb in range(5):
            t0, n = TB[tb]
            si = rr("stg", 2)
            for j in range(4):
                pi = feat_major_unit(cur, j, tb)
                act.op(lambda: nc.scalar.activation(out=stg[si][:, j, 0:n], in_=pacc[pi][:, 0:n], func=AF.Sigmoid),
                       reads=[pacc_b[pi]], writes=[stg_b[si]])
            sp.dma(maT[gi * 512:(gi + 1) * 512, t0:t0 + n].rearrange("(j p) q -> p j q", p=128), stg[si][:, :, 0:n],
                   reads=[stg_b[si]], writes=[io["QT_b"]], sbuf_buf=stg_b[si])
    ck("ma")
    for gi in range(4):
        wi = gi % 2
        pool.dma(wpb[wi][:], w_proj_b[:, gi * 512:(gi + 1) * 512].rearrange("(k p) n -> p k n", p=128), reads=[io["wb_w_proj_b"]], writes=[wpb_b[wi]], sbuf_buf=wpb_b[wi])
        cur = load_group_pref(9216 + gi * 512)
        for tb in range(5):
            t0, n = TB[tb]
            si = rr("stg", 2)
            for j in range(4):
                pi = feat_major_unit(cur, j, tb)
                flip()
                a = tq[0]
                act.op(lambda: nc.scalar.activation(out=a[:, 0:n], in_=pacc[pi][:, 0:n], func=AF.Sigmoid), reads=[pacc_b[pi]], writes=[tq_b[0]])
                mi = rr("ms", 2)
                pe.group([(lambda g=g: nc.tensor.matmul(pms[mi][:, 0:n], wpb[wi][:, g, j * 128:(j + 1) * 128], uT[:, g, t0:t0 + n],
                                                        start=(g == 0), stop=(g == 7))) for g in range(8)],
                         reads=[wpb_b[wi], uT_b], writes=[pms_b[mi]])
                dve.op(lambda: nc.vector.tensor_tensor(out=stg[si][:, j, 0:n], in0=a[:, 0:n], in1=pms[mi][:, 0:n], op=ALU.mult),
                       reads=[tq_b[0], pms_b[mi]], writes=[stg_b[si]])
            sp.dma(tbT[gi * 512:(gi + 1) * 512, t0:t0 + n].rearrange("(j p) q -> p j q", p=128), stg[si][:, :, 0:n],
                   reads=[stg_b[si]], writes=[io["QT_b"]], sbuf_buf=stg_b[si])

    es.close()


def emit_phase_m(kb, io, nlayers):
    nc = kb.nc
    pe, act, dve, pool, sp = kb.pe, kb.act, kb.dve, kb.pool, kb.sp
    es = ExitStack()

    def sb(name, shape, dt):
        return es.enter_context(nc.sbuf_tensor("Msb_" + name, shape, dt))

    MC = 3 * D // NCORES
    cv = sb("cv", [2, D], F32)
    cv_b = kb.buf("cv")
    scT = sb("scT", [128, KC, 2], BF16)
    scT_b = kb.buf("scT")
    ident = sb("ident", [128, 128], F32)
    ident_b = kb.buf("identM")
    wgm = [sb(f"wgm{i}", [128, KC, 384], BF16) for i in range(2)]
    wgm_b = [kb.buf(f"wgm{i}") for i in range(2)]
    bada = sb("bada", [2, nlayers, MC], F32)
    bada_b = kb.buf("bada")
    mo = sb("mo", [2, nlayers, MC], F32)
    mo_b = kb.buf("mo")
    pm = [es.enter_context(nc.psum_tensor(f"Mps{i}", [128, 512], F32)) for i in range(2)]
    pm_b = [kb.buf(f"Mps{i}") for i in range(2)]
    sp.dma(ident[:], io["ident"], writes=[ident_b], sbuf_buf=ident_b)
    sp.dma(cv[:], io["cvec"], writes=[cv_b], sbuf_buf=cv_b)
    for l in range(nlayers):
        sp.dma(bada[:, l, :], io["b_ada_s"][l].rearrange("o n -> (o n)").partition_broadcast(2), writes=[bada_b], sbuf_buf=bada_b)
    act.op(lambda: nc.scalar.activation(out=cv[:], in_=cv[:], func=AF.Silu), reads=[cv_b], writes=[cv_b])
    pe.group([(lambda k=k: nc.tensor.transpose(pm[0][:, 2 * k:2 * k + 2], cv[:, k * 128:(k + 1) * 128], ident[0:2, 0:2]))
              for k in range(KC)], reads=[cv_b, ident_b], writes=[pm_b[0]])
    dve.op(lambda: nc.vector.tensor_copy(out=scT[:].rearrange("p k v -> p (k v)"), in_=pm[0][:, 0:2 * KC]), reads=[pm_b[0]], writes=[scT_b])
    u = 0
    for l in range(nlayers):
        for half in range(2):
            i = u % 2
            u += 1
            pool.dma(wgm[i][:], io["w_ada_s"][l][:, half * 384:(half + 1) * 384].rearrange("(k p) n -> p k n", p=128), writes=[wgm_b[i]], sbuf_buf=wgm_b[i])
            pe.group([(lambda k=k: nc.tensor.matmul(pm[i][0:2, 0:384], scT[:, k, :], wgm[i][:, k, :], start=(k == 0), stop=(k == KC - 1)))
                      for k in range(KC)], reads=[scT_b, wgm_b[i]], writes=[pm_b[i]])
            dve.op(lambda: nc.vector.tensor_tensor(out=mo[:, l, half * 384:(half + 1) * 384], in0=pm[i][0:2, 0:384],
                                                   in1=bada[:, l, half * 384:(half + 1) * 384], op=ALU.add),
                   reads=[pm_b[i], bada_b], writes=[mo_b])
    sp.dma(io["modown"].rearrange("(l v) n -> v l n", v=2), mo[:], reads=[mo_b], writes=[io["modown_b"]], sbuf_buf=mo_b)
    kb.allgather(io["modown"], io["modall"], reads=[io["modown_b"]], writes=[io["modall_b"]])
    es.close()


def emit_phase_b(kb, io, layer, last, pfx="B"):
    nc = kb.nc
    pe, act, dve, pool, sp = kb.pe, kb.act, kb.dve, kb.pool, kb.sp
    lam_init = 0.8 - 0.6 * math.exp(-0.3 * layer)
    QT, KTc, KT_all, Vc, V_all = io["QT"], io["KTc"], io["KT_all"], io["Vc"], io["V_all"]
    gaT, tbT, maT = io["gaT"], io["tbT"], io["maT"]
    nob = io["QT_b"]
    es = ExitStack()

    def sb(name, shape, dt):
        return es.enter_context(nc.sbuf_tensor(pfx + "sb_" + name, shape, dt))

    def ps(name, shape, dt=F32):
        return es.enter_context(nc.psum_tensor(pfx + "ps_" + name, shape, dt))

    kv = sb("kv", [128, 2 * NKEY], BF16)
    kv_b = kb.buf("kv")
    Kh = kv[:, 0:NKEY]
    Vh = kv[:, NKEY:2 * NKEY].rearrange("p (k e) -> p k e", e=128)
    wpa = sb("wpa", [128, 8, D], BF16)
    wpa_b = kb.buf("wpa")
    aT = sb("aT", [128, 8, T], BF16)
    aT_b = kb.buf("aT")
    S1 = sb("S1", [128, 20480], BF16)
    S1f = S1[:].bitcast(F32)
    qh = [S1[:, i * T:(i + 1) * T] for i in range(2)]
    q_b = [kb.buf(f"q{i}") for i in range(2)]
    pT = [S1[:, 4608 + i * 1024:4608 + (i + 1) * 1024] for i in range(3)]
    pT_b = [kb.buf(f"pT{i}") for i in range(3)]
    te = [S1f[:, 4096 + j * 512:4096 + (j + 1) * 512] for j in range(6)]
    te_b = [kb.buf(f"te{j}") for j in range(6)]
    gat = [S1[:, 14336 + i * 512:14336 + (i + 1) * 512] for i in range(2)]
    gat_b = [kb.buf(f"gat{i}") for i in range(2)]
    hx = [sb(f"hx{i}", [128, D], F32) for i in range(2)]
    hx_b = [kb.buf(f"hx{i}") for i in range(2)]
    ones_bf = sb("ones_bf", [128, 32], BF16)
    o32 = sb("o32", [64, 128], F32)
    o128 = sb("o128", [128, 128], F32)
    ones1 = sb("ones1", [1, 128], F32)
    eps_t = sb("eps_t", [128, 1], F32)
    lamv = sb("lamv", [1, 256], F32)
    lamc = sb("lamc", [128, 1], F32)
    sgc = sb("sgc", [128, 1], F32)
    cst_b = kb.buf("cstB")
    lam_b = kb.buf("lam")
    onesf = sb("onesf", [128, 128], F32)
    acc = [S1f[:, 7680 + i * 1024:7680 + (i + 1) * 1024] for i in range(2)]
    acc_b = [kb.buf(f"acc{i}") for i in range(2)]
    S = [ps(f"S{i}", [128, 1024]) for i in range(2)]
    S_b = [kb.buf(f"S{i}") for i in range(2)]
    O = [ps(f"O{i}", [128, 512]) for i in range(2)]
    O_b = kb.buf("O")
    SA = ps("SA", [128, 512])
    X = ps("X", [128, 512])
    X_b = kb.buf("X")

    dve.op(lambda: nc.vector.memset(ones_bf[:], 1.0), writes=[cst_b])
    dve.op(lambda: nc.vector.memset(onesf[:], 1.0), writes=[cst_b])
    dve.op(lambda: nc.vector.memset(o32[:], 1.0 / 32), writes=[cst_b])
    dve.op(lambda: nc.vector.memset(o128[:], 1.0 / 128), writes=[cst_b])
    dve.op(lambda: nc.vector.memset(ones1[:], 1.0), writes=[cst_b])
    dve.op(lambda: nc.vector.memset(eps_t[:], EPS), writes=[cst_b])
    sp.dma(lamv[:], io["lamv"].rearrange("(o a) d -> o (a d)", o=1), writes=[lam_b], sbuf_buf=lam_b)
    with nc.allow_non_contiguous_dma("tiny subln_g column load"):
        sp.dma(sgc[:], io["subln_g"].rearrange("o p -> p o"), writes=[cst_b], sbuf_buf=cst_b)
    dve.op(lambda: nc.vector.tensor_scalar(out=sgc[:], in0=sgc[:], scalar1=1.0 - lam_init, scalar2=None, op0=ALU.mult), reads=[cst_b], writes=[cst_b])
    dve.op(lambda: nc.vector.tensor_tensor(out=lamv[:, 0:128], in0=lamv[:, 0:128], in1=lamv[:, 128:256], op=ALU.mult), reads=[lam_b], writes=[lam_b])
    dve.op(lambda: nc.vector.tensor_reduce(out=lamv[:, 128:130], in_=lamv[:, 0:128].rearrange("o (a d) -> o a d", d=64), axis=AX.X, op=ALU.add),
           reads=[lam_b], writes=[lam_b])
    act.op(lambda: nc.scalar.activation(out=lamv[:, 130:132], in_=lamv[:, 128:130], func=AF.Exp), reads=[lam_b], writes=[lam_b])
    dve.op(lambda: nc.vector.tensor_tensor(out=lamv[:, 132:133], in0=lamv[:, 130:131], in1=lamv[:, 131:132], op=ALU.subtract), reads=[lam_b], writes=[lam_b])
    dve.op(lambda: nc.vector.tensor_scalar(out=lamv[:, 132:133], in0=lamv[:, 132:133], scalar1=lam_init, scalar2=None, op0=ALU.add), reads=[lam_b], writes=[lam_b])
    pe.op(lambda: nc.tensor.matmul(X[:, 0:1], ones1[0:1, :], lamv[0:1, 132:133], start=True, stop=True), reads=[lam_b, cst_b], writes=[X_b])
    dve.op(lambda: nc.vector.tensor_scalar(out=lamc[:], in0=X[:, 0:1], scalar1=-1.0, scalar2=None, op0=ALU.mult), reads=[X_b], writes=[lam_b])

    pool.dma(wpa[:], io["w_proj_a"].rearrange("(k p) n -> p k n", p=128), reads=[io["wb_w_proj_a"]], writes=[wpa_b], sbuf_buf=wpa_b)

    QB = [(0, 256, True)] + [(256 + 512 * j, 512, False) for j in range(4)]
    if last:
        QB = QB[1:]
    cnt = {"u": 0, "g": 0, "m": 0, "h": 0}

    def pv(kc, pi, n, first, lastc):
        fns = []
        for m in range(2):
            fns.append(lambda m=m: nc.tensor.matmul(O[m][:, 0:n], Vh[:, kc, :], pT[pi][:, m * 512:m * 512 + n], start=first, stop=lastc))
        pe.group(fns, reads=[kv_b, pT_b[pi], cst_b], writes=[O_b])

    for h in range(8):
        with nc.allow_non_contiguous_dma("kv head loads"):
            sp.dma(Kh[:, 0:CTX], KTc[h], reads=[io["kvall_b"]], writes=[kv_b], sbuf_buf=kv_b)
            sp.dma(Kh[:, CTX:NKEY].rearrange("p (c q) -> p c q", c=NCORES), KT_all[:, h].rearrange("c p q -> p c q"),
                   reads=[io["kvall_b"]], writes=[kv_b], sbuf_buf=kv_b)
            sp.dma(Vh[:, 0:2, :], Vc[h], reads=[io["kvall_b"]], writes=[kv_b], sbuf_buf=kv_b)
            sp.dma(Vh[:, 2:NKC, :].rearrange("p (c k) e -> p c (k e)", c=NCORES), V_all[:, h].rearrange("c p k e -> p c (k e)"),
                   reads=[io["kvall_b"]], writes=[kv_b], sbuf_buf=kv_b)
        qi = h % 2
        sp.dma(qh[qi], QT[h], writes=[q_b[qi]], sbuf_buf=q_b[qi])
        for (q0, n, isctx) in QB:
            chunks = [0, 1] if isctx else list(range(NKC))
            gi = cnt["g"] % 2
            cnt["g"] += 1
            sp.dma(gat[gi][:, 0:n], gaT[h * 128:(h + 1) * 128, q0:q0 + n], writes=[gat_b[gi]], sbuf_buf=gat_b[gi])
            prev = None
            for ci, kc in enumerate(chunks):
                si = cnt["u"] % 2
                pi = cnt["u"] % 3
                cnt["u"] += 1
                pe.group([(lambda m=m: nc.tensor.matmul(S[si][:, m * 512:m * 512 + n], Kh[64 * m:64 * m + 64, kc * 128:(kc + 1) * 128],
                                                        qh[qi][64 * m:64 * m + 64, q0:q0 + n], start=True, stop=True, tile_position=(64 * m, 0)))
                          for m in range(2)], reads=[kv_b, q_b[qi]], writes=[S_b[si]])
                act.op(lambda: nc.scalar.activation(out=pT[pi].rearrange("p (m q) -> p m q", m=2)[:, :, 0:n],
                                                    in_=S[si][:].rearrange("p (m q) -> p m q", m=2)[:, :, 0:n], func=AF.Exp, scale=0.125),
                       reads=[S_b[si]], writes=[pT_b[pi]])
                ai = ci % 2
                pv_ = pT[pi].rearrange("p (m q) -> p m q", m=2)[:, :, 0:n]
                av_ = acc[ai].rearrange("p (m q) -> p m q", m=2)[:, :, 0:n]
                if ci < 2:
                    dve.op(lambda: nc.vector.tensor_copy(out=av_, in_=pv_), reads=[pT_b[pi]], writes=[acc_b[ai]])
                else:
                    dve.op(lambda: nc.vector.tensor_tensor(out=av_, in0=av_, in1=pv_, op=ALU.add), reads=[pT_b[pi], acc_b[ai]], writes=[acc_b[ai]])
                if prev is not None:
                    pv(*prev)
                prev = (kc, pi, n, ci == 0, ci == len(chunks) - 1)
            pv(*prev)
            for m in range(2):
                pe.group([lambda: nc.tensor.matmul(X[:, 0:n], onesf[:, :], acc[0][:, m * 512:m * 512 + n], start=True, stop=False),
                          lambda: nc.tensor.matmul(X[:, 0:n], onesf[:, :], acc[1][:, m * 512:m * 512 + n], start=False, stop=True)],
                         reads=[acc_b[0], acc_b[1], cst_b], writes=[X_b])
                dve.op(lambda: nc.vector.reciprocal(out=te[1 + m][:, 0:n], in_=X[:, 0:n]), reads=[X_b], writes=[te_b[1 + m]])
            dve.op(lambda: nc.vector.tensor_tensor(out=te[3][:, 0:n], in0=O[0][:, 0:n], in1=te[1][:, 0:n], op=ALU.mult), reads=[O_b, te_b[1]], writes=[te_b[3]])
            dve.op(lambda: nc.vector.tensor_tensor(out=te[4][:, 0:n], in0=O[1][:, 0:n], in1=te[2][:, 0:n], op=ALU.mult), reads=[O_b, te_b[2]], writes=[te_b[4]])
            dve.op(lambda: nc.vector.scalar_tensor_tensor(out=te[3][:, 0:n], in0=te[4][:, 0:n], scalar=lamc[:, 0:1], in1=te[3][:, 0:n],
                                                          op0=ALU.mult, op1=ALU.add), reads=[te_b[3], te_b[4], lam_b], writes=[te_b[3]])
            act.op(lambda: nc.scalar.activation(out=te[1][:, 0:n], in_=te[3][:, 0:n], func=AF.Square), reads=[te_b[3]], writes=[te_b[1]])
            pe.op(lambda: nc.tensor.matmul(X[:, 0:n], o128[:, :], te[1][:, 0:n], start=True, stop=True), reads=[te_b[1], cst_b], writes=[X_b])
            act.op(lambda: nc.scalar.activation(out=te[2][:, 0:n], in_=X[:, 0:n], func=AF.Sqrt, bias=eps_t[:], scale=1.0), reads=[X_b, cst_b], writes=[te_b[2]])
            dve.op(lambda: nc.vector.reciprocal(out=te[2][:, 0:n], in_=te[2][:, 0:n]), reads=[te_b[2]], writes=[te_b[2]])
            dve.op(lambda: nc.vector.tensor_tensor(out=te[3][:, 0:n], in0=te[3][:, 0:n], in1=te[2][:, 0:n], op=ALU.mult), reads=[te_b[3], te_b[2]], writes=[te_b[3]])
            dve.op(lambda: nc.vector.scalar_tensor_tensor(out=aT[:, h, q0:q0 + n], in0=te[3][:, 0:n], scalar=sgc[:, 0:1], in1=gat[gi][:, 0:n],
                                                          op0=ALU.mult, op1=ALU.mult), reads=[te_b[3], gat_b[gi], cst_b], writes=[aT_b])

    kb.barrier()
    wo = kv[:, 0:KC * D].rearrange("p (k n) -> p k n", k=KC)
    pool.dma(wo, io["w_out"].rearrange("(k p) n -> p k n", p=128), reads=[io["wb_w_out"]], writes=[kv_b], sbuf_buf=kv_b)
    gate_bc = S1f[:, 0:2 * D].rearrange("p (v n) -> p v n", v=2)
    gate_b = kb.buf("gate")
    modall = io["modall"]
    for v in range(2):
        for (c_, lo, hi, d0) in ((5, 256, 768, 0), (6, 0, 768, 512), (7, 0, 768, 1280)):
            r_ = c_ * 2 * DEPTH + layer * 2 + v
            sp.dma(gate_bc[:, v, d0:d0 + hi - lo], modall[r_, lo:hi].partition_broadcast(128), reads=[io["modall_b"]], writes=[gate_b], sbuf_buf=gate_b)
    yT = S1[:, 8192:16384].rearrange("p (k q) -> p k q", k=KC)
    yT_b = kb.buf("yT")
    mt = [S1[:, 16384 + i * 512:16384 + (i + 1) * 512] for i in range(2)]
    tt = [S1[:, 17408 + i * 512:17408 + (i + 1) * 512] for i in range(2)]
    mt_b = [kb.buf(f"mt{i}") for i in range(2)]
    tc_ = [sb(f"tc{i}", [128, 512], F32) for i in range(2)]
    tc_b = [kb.buf(f"tc{i}") for i in range(2)]
    Oc_b = [kb.buf("Oc0"), kb.buf("Oc1")]
    for (q0, n, isctx) in QB:
        for dc in range(KC):
            mi = cnt["m"] % 2
            cnt["m"] += 1
            sp.dma(mt[mi][:, 0:n], maT[dc * 128:(dc + 1) * 128, q0:q0 + n], writes=[mt_b[mi]], sbuf_buf=mt_b[mi])
            sp.dma(tt[mi][:, 0:n], tbT[dc * 128:(dc + 1) * 128, q0:q0 + n], writes=[mt_b[mi]], sbuf_buf=mt_b[mi])
            oi = dc % 2
            pe.group([(lambda hh=hh: nc.tensor.matmul(O[oi][:, 0:n], wpa[:, hh, dc * 128:(dc + 1) * 128], aT[:, hh, q0:q0 + n],
                                                      start=(hh == 0), stop=(hh == 7))) for hh in range(8)],
                     reads=[wpa_b, aT_b], writes=[Oc_b[oi]])
            dve.op(lambda: nc.vector.tensor_tensor(out=tc_[mi][:, 0:n], in0=O[oi][:, 0:n], in1=mt[mi][:, 0:n], op=ALU.mult),
                   reads=[Oc_b[oi], mt_b[mi]], writes=[tc_b[mi]])
            dve.op(lambda: nc.vector.tensor_tensor(out=yT[:, dc, 0:n], in0=tc_[mi][:, 0:n], in1=tt[mi][:, 0:n], op=ALU.add),
                   reads=[tc_b[mi], mt_b[mi]], writes=[yT_b])
        for tile in range(n // 128):
            tok0 = q0 + tile * 128
            var = 1 if tok0 < CTX else 0
            hi = cnt["h"] % 2
            cnt["h"] += 1
            sp.dma(hx[hi][:], io["hin"][tok0:tok0 + 128, :], writes=[hx_b[hi]], sbuf_buf=hx_b[hi])
            for nb in range(4):
                si = nb % 2
                pe.group([(lambda dc=dc: nc.tensor.matmul(S[si][:, 0:512], yT[:, dc, tile * 128:(tile + 1) * 128], wo[:, dc, nb * 512:(nb + 1) * 512],
                                                          start=(dc == 0), stop=(dc == KC - 1))) for dc in range(KC)],
                         reads=[yT_b, kv_b], writes=[S_b[si]])
                ti = nb % 2
                dve.op(lambda: nc.vector.tensor_tensor(out=tc_[ti][:], in0=S[si][:, 0:512], in1=gate_bc[:, var, nb * 512:(nb + 1) * 512], op=ALU.mult),
                       reads=[S_b[si], gate_b], writes=[tc_b[ti]])
                dve.op(lambda: nc.vector.tensor_tensor(out=hx[hi][:, nb * 512:(nb + 1) * 512], in0=hx[hi][:, nb * 512:(nb + 1) * 512], in1=tc_[ti][:], op=ALU.add),
                       reads=[tc_b[ti], hx_b[hi]], writes=[hx_b[hi]])
            if last:
                dst = io["y"][tok0 - CTX:tok0 - CTX + 128, :]
            else:
                dst = io["hout"][tok0:tok0 + 128, :]
            sp.dma(dst, hx[hi][:], reads=[hx_b[hi]], sbuf_buf=hx_b[hi])
    es.close()


def build_fused(nlayers=DEPTH):
    kb = KB()
    nc = kb.nc
    io0 = {}
    hin0 = kb.dram_in("hin", [T, D])
    cvec = kb.dram_in("cvec", [2, D])
    identd = kb.dram_in("ident", [128, 128])
    cosd = kb.dram_in("cos", [T, 32])
    sind = kb.dram_in("sin", [T, 32])
    y = kb.dram_out("y", [TOK, D])

    def internal(name, shape, dt=F32):
        return nc.dram_tensor(name, list(shape), dt)

    hbuf = internal("hbuf", [T, D]).ap()
    QT = internal("QT", [8, 128, T], BF16).ap()
    KTc = internal("KTc", [8, 128, CTX], BF16).ap()
    KTo = internal("KTo", [8, 128, TOK], BF16)
    KTa = internal("KTa", [NCORES * 8, 128, TOK], BF16)
    Vc = internal("Vc", [8, 128, 2, 128], BF16).ap()
    Vo = internal("Vo", [8, 128, 16, 128], BF16)
    Va = internal("Va", [NCORES * 8, 128, 16, 128], BF16)
    gaT = internal("gaT", [1024, T], BF16).ap()
    tbT = internal("tbT", [D, T], BF16).ap()
    maT = internal("maT", [D, T], BF16).ap()
    nob = kb.buf("nob")
    nob.untracked = True
    kvall_b = kb.buf("kvall")
    kvown_b = kb.buf("kvown")
    kvown_b.multi = True
    wspec = [("w_in", D, INW), ("w_proj_b", 1024, D), ("w_proj_a", 1024, D), ("w_out", D, D)]
    layers = []
    wsets = [{nm: internal(f"{nm}_f{i}", [r, c]) for nm, r, c in wspec} for i in range(2)]
    wset_b = [{nm: kb.buf(f"wset{i}{nm}") for nm, _, _ in wspec} for i in range(2)]
    for l in range(nlayers):
        L = {"w_b": wset_b[l % 2], "bn_b": {nm: kb.buf(f"bn{l}{nm}") for nm, _, _ in wspec}}
        for nm, r, c in wspec:
            shard = kb.dram_in(f"{nm}_s{l}", [r // NCORES, c])
            bounce = internal(f"{nm}_bn{l}", [r // NCORES, c])
            full = wsets[l % 2][nm]
            L[nm] = (bounce, full, shard)
        for nm, shp in [("norm_g", [1, D]), ("qk_g", [2, 64]), ("v_norm_g", [8, 128]), ("w_spatial", [8, 128, 128]),
                        ("b_spatial", [8, 128]), ("lamv", [4, 64]), ("subln_g", [1, 128])]:
            L[nm] = kb.dram_in(f"{nm}{l}", shp)
        layers.append(L)
    def bounce_weights(l):
        L = layers[l]
        for nm, r, c in wspec:
            bounce, full, shard = L[nm]
            kb.sp.dma(bounce.ap(), shard, writes=[L["bn_b"][nm]], sbuf_buf=L["bn_b"][nm])

    def gather_weights(l):
        L = layers[l]
        for nm, r, c in wspec:
            bounce, full, shard = L[nm]
            kb.allgather(bounce.ap(), full.ap(), reads=[L["bn_b"][nm]], writes=[L["w_b"][nm]])

    MC = 3 * D // NCORES
    modown = internal("modown", [2 * nlayers, MC]).ap()
    modall = internal("modall", [NCORES * 2 * nlayers, MC]).ap()
    modown_b = kb.buf("modown")
    modall_b = kb.buf("modall")
    iom = dict(ident=identd, cvec=cvec, modown=modown, modall=modall, modown_b=modown_b, modall_b=modall_b,
               w_ada_s=[kb.dram_in(f"w_ada_c{l}", [D, MC]) for l in range(nlayers)],
               b_ada_s=[kb.dram_in(f"b_ada_c{l}", [1, MC]) for l in range(nlayers)])
    bounce_weights(0)
    gather_weights(0)
    emit_phase_m(kb, iom, nlayers)
    kb.barrier()
    for l in range(1, nlayers):
        bounce_weights(l)
    for l in range(1, min(2, nlayers)):
        gather_weights(l)
    for l in range(nlayers):
        L = layers[l]
        last = (l == nlayers - 1)
        io = dict(hin=hin0 if l == 0 else hbuf, hout=hbuf, y=y, ident=identd, cos=cosd, sin=sind, layer=l, modall=modall, modall_b=modall_b,
                  QT=QT, KTc=KTc, KT=KTo.ap(), V=Vo.ap(), Vc=Vc, gaT=gaT, tbT=tbT, maT=maT,
                  QT_b=nob, KV_b=nob, h_b=nob, kvall_b=kvall_b,
                  wb_w_in=L["w_b"]["w_in"], wb_w_proj_b=L["w_b"]["w_proj_b"],
                  wb_w_proj_a=L["w_b"]["w_proj_a"], wb_w_out=L["w_b"]["w_out"],
                  KT_all=KTa.ap().rearrange("(c h) p q -> c h p q", c=NCORES), V_all=Va.ap().rearrange("(c h) p k e -> c h p k e", c=NCORES))
        for nm, _, _ in wspec:
            io[nm] = L[nm][1].ap()
        for nm in ("norm_g", "qk_g", "v_norm_g", "w_spatial", "b_spatial", "lamv", "subln_g"):
            io[nm] = L[nm]
        def after_kv():
            kb.allgather(KTo.ap().rearrange("h p q -> (h p) q"), KTa.ap().rearrange("h p q -> (h p) q"), reads=[kvown_b], writes=[kvall_b])
            kb.allgather(Vo.ap().rearrange("h p k e -> (h p) (k e)"), Va.ap().rearrange("h p k e -> (h p) (k e)"), reads=[kvown_b], writes=[kvall_b])

        io["after_kv"] = after_kv
        io["KV_b"] = kvown_b
        emit_phase_a(kb, io, None, pfx=f"A{l}")
        kb.barrier()
        emit_phase_b(kb, io, l, last, pfx=f"B{l}")
        kb.barrier()
        if l + 2 < nlayers:
            gather_weights(l + 2)
    kb.sp.wait_all([b for b in kb.bufs])
    for sm in kb.all_sems:
        if sm.cnt > kb.sp.waited.get(sm, 0):
            kb.sp.eng.wait_ge(sm.h, sm.cnt)
    nc.all_engine_barrier()
    kb.es.close()
    return nc


_NC_CACHE = {}


def kernel(x, c, ctx, c_ctx, w_ada, b_ada, norm_g, w_in, q_norm_g, k_norm_g, lam_q1, lam_k1, lam_q2, lam_k2, subln_g,
           v_norm_g, w_spatial, b_spatial, w_proj_a, w_proj_b, w_out):
    f = lambda a: np.ascontiguousarray(np.asarray(a, dtype=np.float32))
    x, c, ctx, c_ctx = f(x), f(c), f(ctx), f(c_ctx)
    W = dict(w_in=f(w_in), w_proj_b=f(w_proj_b), w_proj_a=f(w_proj_a), w_out=f(w_out))
    w_ada, b_ada = f(w_ada), f(b_ada)
    MC = 3 * D // NCORES
    if "nc" not in _NC_CACHE:
        _NC_CACHE["nc"] = build_fused()
    nc = _NC_CACHE["nc"]
    tabs = rope_tables()
    ident = np.eye(128, dtype=np.float32)
    cvec = np.ascontiguousarray(np.stack([c[0], c_ctx], 0))
    in_maps = []
    for core in range(NCORES):
        m = dict(hin=np.ascontiguousarray(np.concatenate([ctx[0], x[0, core * TOK:(core + 1) * TOK]], 0)), cvec=cvec, ident=ident,
                 cos=tabs[core][0], sin=tabs[core][1])
        for l in range(DEPTH):
            for nm, arr in W.items():
                r = arr.shape[1] // NCORES
                m[f"{nm}_s{l}"] = np.ascontiguousarray(arr[l, core * r:(core + 1) * r])
            m[f"w_ada_c{l}"] = np.ascontiguousarray(w_ada[l][:, core * MC:(core + 1) * MC])
            m[f"b_ada_c{l}"] = np.ascontiguousarray(b_ada[l][None, core * MC:(core + 1) * MC])
            m[f"norm_g{l}"] = f(norm_g)[l][None]
            m[f"qk_g{l}"] = np.ascontiguousarray(np.stack([f(q_norm_g)[l], f(k_norm_g)[l]], 0))
            m[f"v_norm_g{l}"] = f(v_norm_g)[l]
            m[f"w_spatial{l}"] = f(w_spatial)[l]
            m[f"b_spatial{l}"] = f(b_spatial)[l]
            m[f"lamv{l}"] = np.ascontiguousarray(np.stack([f(lam_q1)[l], f(lam_q2)[l], f(lam_k1)[l], f(lam_k2)[l]], 0))
            m[f"subln_g{l}"] = f(subln_g)[l][None]
        in_maps.append(m)
    res = run_bass_kernel_spmd(nc, in_maps, core_ids=list(range(NCORES)))
    out = np.concatenate([np.asarray(res.results[core]["y"]) for core in range(NCORES)], 0)
    return out[None].astype(np.float32)


def pool_or_dve(kb):
    return kb.dve


def build_a(stop_after=None):
    kb = KB()
    io = {}
    io["hin"] = kb.dram_in("hin", [T, D])
    io["cvec"] = kb.dram_in("cvec", [2, D])
    io["w_ada"] = kb.dram_in("w_ada", [D, 3 * D])
    io["b_ada"] = kb.dram_in("b_ada", [1, 3 * D])
    io["norm_g"] = kb.dram_in("norm_g", [1, D])
    io["w_in"] = kb.dram_in("w_in", [D, INW])
    io["qk_g"] = kb.dram_in("qk_g", [2, 64])
    io["v_norm_g"] = kb.dram_in("v_norm_g", [8, 128])
    io["w_spatial"] = kb.dram_in("w_spatial", [8, 128, 128])
    io["b_spatial"] = kb.dram_in("b_spatial", [8, 128])
    io["w_proj_b"] = kb.dram_in("w_proj_b", [1024, D])
    io["ident"] = kb.dram_in("ident", [128, 128])
    io["cos"] = kb.dram_in("cos", [T, 32])
    io["sin"] = kb.dram_in("sin", [T, 32])
    io["QT"] = kb.dram_out("QT", [8, 128, T], BF16)
    io["KT"] = kb.dram_out("KT", [8, 128, T], BF16)
    io["V"] = kb.dram_out("V", [T, 1024], BF16)
    io["gaT"] = kb.dram_out("gaT", [1024, T], BF16)
    io["tbT"] = kb.dram_out("tbT", [D, T], BF16)
    io["maT"] = kb.dram_out("maT", [D, T], BF16)
    io["modrows"] = kb.dram_out("modrows", [2, 3 * D])
    try:
        emit_phase_a(kb, io, stop_after)
    except StopEmit:
        pass
    kb.sp.wait_all(kb.bufs)
    kb.nc.all_engine_barrier()
    kb.es.close()
    return kb.nc


def rope_tables():
    inv = (10000.0 ** (-np.arange(16, dtype=np.float32) / 16)).astype(np.float32)
    t = np.arange(SEQ)
    row = (t // GRID_W).astype(np.float32)
    col = (t % GRID_W).astype(np.float32)
    ang = np.stack([row[:, None] * inv, col[:, None] * inv], axis=1).astype(np.float32)
    cos = np.cos(ang).astype(np.float32).reshape(SEQ, 32)
    sin = np.sin(ang).astype(np.float32).reshape(SEQ, 32)
    out = []
    for c in range(NCORES):
        cc = np.concatenate([np.ones((CTX, 32), np.float32), cos[c * TOK:(c + 1) * TOK]], 0)
        ss = np.concatenate([np.zeros((CTX, 32), np.float32), sin[c * TOK:(c + 1) * TOK]], 0)
        out.append((np.ascontiguousarray(cc), np.ascontiguousarray(ss)))
    return out
```
